# Optimizing a Trainium2 kernel written in Bass

```python
import math
import jax, jax.numpy as jnp
from jax import lax
import numpy as np


D_MODEL = 1024
BATCH = 2
SEQ = 8192
DEPTH = 4

GRID_W = 64
CTX_LEN = 256
LN_EPS = 1e-5

HEAD_DIM = 64
N_HEADS_A = D_MODEL // (2 * HEAD_DIM)
V_DIM = 2 * HEAD_DIM
QK_W = N_HEADS_A * 2 * HEAD_DIM
V_W = N_HEADS_A * V_DIM
ATTN_OUT = V_W
ATTN_SCALE = HEAD_DIM ** -0.5
Q_BLOCK = 128
ROPE_BASE = 10000.0
ROPE_AXIS_PAIRS = HEAD_DIM // 4

CHUNK = 128
SGU_GROUPS = 4
SGU_W = D_MODEL // 2
SGU_GC = SGU_W // SGU_GROUPS

AB_IN = 2 * QK_W + V_W + 2 * SGU_W
AB_OUT = ATTN_OUT + SGU_W
AB_SPLITS = [QK_W, 2 * QK_W, 2 * QK_W + V_W, 2 * QK_W + V_W + SGU_W]

POOL_WINDOWS = (2, 4, 8, 16)
N_POOL_GROUPS = len(POOL_WINDOWS)
POOL_W = D_MODEL
POOL_GC = POOL_W // N_POOL_GROUPS

P_HEADS = 8
N_KEYS = 128
N_EXPERTS = N_KEYS * N_KEYS
P_TOPK = 16
D_KEY = 256
D_KEY_HALF = D_KEY // 2
PEER_BLOCK = 128

kernel_name = 'hybrid_diffattn_sgu_pool_peer_trunk'


def layer_norm(h, g, b):
    h32 = h.astype(jnp.float32)
    mu = jnp.mean(h32, axis=-1, keepdims=True)
    var = jnp.mean(jnp.square(h32 - mu), axis=-1, keepdims=True)
    out = (h32 - mu) * lax.rsqrt(var + LN_EPS) * g.astype(jnp.float32) + b.astype(jnp.float32)
    return out.astype(h.dtype)


def modulate(h, shift, scale):
    return h * (1.0 + scale) + shift


def axial_rope_tables(rows):
    row = jnp.repeat(jnp.arange(rows, dtype=jnp.float32), GRID_W)
    col = jnp.tile(jnp.arange(GRID_W, dtype=jnp.float32), rows)
    inv = ROPE_BASE ** (-jnp.arange(ROPE_AXIS_PAIRS, dtype=jnp.float32) / ROPE_AXIS_PAIRS)
    ang = jnp.concatenate([row[:, None] * inv, col[:, None] * inv], axis=-1)
    return jnp.cos(ang), jnp.sin(ang)


def apply_rope(t, cos, sin):
    t32 = t.astype(jnp.float32)
    half = HEAD_DIM // 2
    a, b = t32[..., :half], t32[..., half:]
    cs = cos[None, :, None, None, :]
    sn = sin[None, :, None, None, :]
    return jnp.concatenate([a * cs - b * sn, a * sn + b * cs], axis=-1).astype(t.dtype)


def diff_attn_block(q, k, v, lam):
    s = jnp.einsum('bqhmd,bkhmd->bhmqk', q, k).astype(jnp.float32) * ATTN_SCALE
    p = jax.nn.softmax(s, axis=-1)
    a = (p[:, :, 0] - lam * p[:, :, 1]).astype(v.dtype)
    return jnp.einsum('bhqk,bkhe->bqhe', a, v)


def head_rms(o, g, lam_init):
    o32 = o.astype(jnp.float32)
    o32 = o32 * lax.rsqrt(jnp.mean(o32 * o32, axis=-1, keepdims=True) + LN_EPS)
    return (o32 * g.astype(jnp.float32) * (1.0 - lam_init)).astype(o.dtype)


def spatial_gate(gu, gv, ln_g, ln_b, w_s, b_s):
    u = jax.nn.gelu(gu)
    v = layer_norm(jax.nn.gelu(gv), ln_g, ln_b)
    bsz, t, _ = v.shape
    v = v.reshape(bsz, t // CHUNK, CHUNK, SGU_GROUPS, SGU_GC)
    s = jnp.einsum('gpq,bnqgc->bnpgc', w_s, v) + b_s.T[:, :, None]
    return u * s.reshape(bsz, t, SGU_W)


def mixer_ab(hx, hc, w_in, w_out, lam_vec, norm_g, sgu_ln_g, sgu_ln_b, sgu_w, sgu_b,
             lam_init, cos, sin, ctx_out):
    bsz, s_len, _ = hx.shape
    c_len = hc.shape[1]
    lq1, lk1, lq2, lk2 = lam_vec.astype(jnp.float32)
    lam = jnp.exp(jnp.sum(lq1 * lk1)) - jnp.exp(jnp.sum(lq2 * lk2)) + lam_init
    q_x, k_x, v_x, gu_x, gv_x = jnp.split(hx @ w_in, AB_SPLITS, axis=-1)
    q_x = apply_rope(q_x.reshape(bsz, s_len, N_HEADS_A, 2, HEAD_DIM), cos, sin)
    k_x = apply_rope(k_x.reshape(bsz, s_len, N_HEADS_A, 2, HEAD_DIM), cos, sin)
    v_x = v_x.reshape(bsz, s_len, N_HEADS_A, V_DIM)
    if ctx_out:
        q_c, k_c, v_c, gu_c, gv_c = jnp.split(hc @ w_in, AB_SPLITS, axis=-1)
    else:
        k_c, v_c = jnp.split(hc @ w_in[:, QK_W:2 * QK_W + V_W], [QK_W], axis=-1)
    k_c = k_c.reshape(bsz, c_len, N_HEADS_A, 2, HEAD_DIM)
    v_c = v_c.reshape(bsz, c_len, N_HEADS_A, V_DIM)
    k_all = jnp.concatenate([k_c, k_x], axis=1)
    v_all = jnp.concatenate([v_c, v_x], axis=1)
    n_blk = s_len // Q_BLOCK
    qb = jnp.moveaxis(q_x.reshape(bsz, n_blk, Q_BLOCK, N_HEADS_A, 2, HEAD_DIM), 1, 0)
    o_x = lax.map(lambda qblk: diff_attn_block(qblk, k_all, v_all, lam), qb)
    o_x = jnp.moveaxis(o_x, 0, 1).reshape(bsz, s_len, N_HEADS_A, V_DIM)
    a_x = head_rms(o_x, norm_g, lam_init).reshape(bsz, s_len, ATTN_OUT)
    g_x = spatial_gate(gu_x, gv_x, sgu_ln_g, sgu_ln_b, sgu_w, sgu_b)
    y_x = jnp.concatenate([a_x, g_x], axis=-1) @ w_out
    if not ctx_out:
        return y_x, None
    q_c = q_c.reshape(bsz, c_len, N_HEADS_A, 2, HEAD_DIM)
    o_c = diff_attn_block(q_c, k_c, v_c, lam)
    a_c = head_rms(o_c, norm_g, lam_init).reshape(bsz, c_len, ATTN_OUT)
    g_c = spatial_gate(gu_c, gv_c, sgu_ln_g, sgu_ln_b, sgu_w, sgu_b)
    y_c = jnp.concatenate([a_c, g_c], axis=-1) @ w_out
    return y_x, y_c


def multiscale_pool(h):
    bsz, t, _ = h.shape
    h32 = h.astype(jnp.float32)
    cs = jnp.concatenate([jnp.zeros((bsz, 1, POOL_W), jnp.float32), jnp.cumsum(h32, axis=1)], axis=1)
    pos = jnp.arange(t)
    outs = []
    for g, w in enumerate(POOL_WINDOWS):
        lo = jnp.clip(pos - w // 2, 0, t)
        hi = jnp.clip(pos + (w - w // 2), 0, t)
        seg = cs[:, :, g * POOL_GC:(g + 1) * POOL_GC]
        cnt = (hi - lo).astype(jnp.float32)[None, :, None]
        outs.append((seg[:, hi] - seg[:, lo]) / cnt)
    pooled = jnp.concatenate(outs, axis=-1)
    return (pooled - h32).astype(h.dtype)


def mixer_pool(h, w_in, w_grp, scale, w_out):
    m = multiscale_pool(h @ w_in)
    bsz, t, _ = m.shape
    m = jnp.einsum('btgc,gce->btge', m.reshape(bsz, t, N_POOL_GROUPS, POOL_GC), w_grp)
    return (m.reshape(bsz, t, POOL_W) * scale) @ w_out


def peer(h, wq, keys, u_tab, v_tab):
    n, d = h.shape
    hb = h.reshape(n // PEER_BLOCK, PEER_BLOCK, d)

    def block(hblk):
        q = (hblk @ wq).reshape(PEER_BLOCK, P_HEADS, 2, D_KEY_HALF)
        s = jnp.einsum('nhpd,hpkd->nhpk', q, keys).astype(jnp.float32)
        top_s, top_i = lax.top_k(s, P_TOPK)
        cand_s = (top_s[:, :, 0, :, None] + top_s[:, :, 1, None, :]).reshape(PEER_BLOCK, P_HEADS, P_TOPK * P_TOPK)
        cand_i = (top_i[:, :, 0, :, None] * N_KEYS + top_i[:, :, 1, None, :]).reshape(PEER_BLOCK, P_HEADS, P_TOPK * P_TOPK)
        best_s, best_pos = lax.top_k(cand_s, P_TOPK)
        idx = jnp.take_along_axis(cand_i, best_pos, axis=-1)
        gate = jax.nn.softmax(best_s, axis=-1)
        u = u_tab[idx]
        v = v_tab[idx]
        act = jax.nn.gelu(jnp.einsum('nd,nhkd->nhk', hblk, u))
        return jnp.einsum('nhk,nhkd->nd', (gate * act).astype(v.dtype), v)

    return lax.map(block, hb).reshape(n, d)


def setup_inputs(seed: int = 0) -> dict:
    key = jax.random.key(seed)
    ks = jax.random.split(key, 24)
    n_even = (DEPTH + 1) // 2
    n_odd = DEPTH // 2
    beta = (8.0 * DEPTH) ** -0.25

    def nrm(k, shape, s):
        return jax.random.normal(k, shape, jnp.float32) * s

    return {
        'x': nrm(ks[0], (BATCH, SEQ, D_MODEL), 1.0),
        'c': nrm(ks[1], (BATCH, D_MODEL), 1.0),
        'ctx': nrm(ks[2], (BATCH, CTX_LEN, D_MODEL), 1.0),
        'c_ctx': nrm(ks[3], (D_MODEL,), 1.0),
        'ada_w': nrm(ks[4], (DEPTH, D_MODEL, 6 * D_MODEL), 0.5 * D_MODEL ** -0.5),
        'ada_b': nrm(ks[5], (DEPTH, 6 * D_MODEL), 0.02),
        'ln_g': 1.0 + nrm(ks[6], (DEPTH, 2, D_MODEL), 0.05),
        'ln_b': nrm(ks[7], (DEPTH, 2, D_MODEL), 0.02),
        'ab_w_in': nrm(ks[8], (n_even, D_MODEL, AB_IN), D_MODEL ** -0.5),
        'ab_w_out': nrm(ks[9], (n_even, AB_OUT, D_MODEL), beta * AB_OUT ** -0.5),
        'diff_lam': nrm(ks[10], (n_even, 4, HEAD_DIM), 0.1),
        'diff_norm_g': 1.0 + nrm(ks[11], (n_even, V_DIM), 0.05),
        'sgu_ln_g': 1.0 + nrm(ks[12], (n_even, SGU_W), 0.05),
        'sgu_ln_b': nrm(ks[13], (n_even, SGU_W), 0.02),
        'sgu_w': nrm(ks[14], (n_even, SGU_GROUPS, CHUNK, CHUNK), CHUNK ** -0.5),
        'sgu_b': 1.0 + nrm(ks[15], (n_even, SGU_GROUPS, CHUNK), 0.1),
        'pool_w_in': nrm(ks[16], (n_odd, D_MODEL, POOL_W), D_MODEL ** -0.5),
        'pool_w_grp': nrm(ks[17], (n_odd, N_POOL_GROUPS, POOL_GC, POOL_GC), POOL_GC ** -0.5),
        'pool_scale': 1.0 + nrm(ks[18], (n_odd, POOL_W), 0.1),
        'pool_w_out': nrm(ks[19], (n_odd, POOL_W, D_MODEL), beta * POOL_W ** -0.5),
        'peer_wq': nrm(ks[20], (DEPTH, D_MODEL, P_HEADS * D_KEY), D_MODEL ** -0.5),
        'peer_keys': nrm(ks[21], (DEPTH, P_HEADS, 2, N_KEYS, D_KEY_HALF), D_KEY_HALF ** -0.5),
        'peer_u': nrm(ks[22], (DEPTH, N_EXPERTS, D_MODEL), D_MODEL ** -0.5),
        'peer_v': nrm(ks[23], (DEPTH, N_EXPERTS, D_MODEL), beta),
    }


def reference(x, c, ctx, c_ctx, ada_w, ada_b, ln_g, ln_b, ab_w_in, ab_w_out, diff_lam,
              diff_norm_g, sgu_ln_g, sgu_ln_b, sgu_w, sgu_b, pool_w_in, pool_w_grp,
              pool_scale, pool_w_out, peer_wq, peer_keys, peer_u, peer_v):
    bsz, s_len, d = x.shape
    c_len = ctx.shape[1]
    rows = s_len // GRID_W
    cos, sin = axial_rope_tables(rows)
    alpha = (2.0 * DEPTH) ** 0.25
    last_ctx_read = 2 * ((DEPTH - 1) // 2)
    silu_c = jax.nn.silu(c)
    silu_cc = jax.nn.silu(c_ctx)
    xs, cs = x, ctx
    for i in range(DEPTH):
        j = i // 2
        even = i % 2 == 0
        ctx_out = i < last_ctx_read
        sh1, sc1, g1, sh2, sc2, g2 = jnp.split((silu_c @ ada_w[i] + ada_b[i])[:, None, :], 6, axis=-1)
        hx = modulate(xs, sh1, sc1)
        if ctx_out or even:
            csh1, csc1, cg1, csh2, csc2, cg2 = jnp.split(silu_cc @ ada_w[i] + ada_b[i], 6, axis=-1)
            hc = modulate(cs, csh1, csc1)
        if even:
            lam_init = 0.8 - 0.6 * math.exp(-0.3 * i)
            yx, yc = mixer_ab(hx, hc, ab_w_in[j], ab_w_out[j], diff_lam[j], diff_norm_g[j],
                              sgu_ln_g[j], sgu_ln_b[j], sgu_w[j], sgu_b[j], lam_init, cos, sin, ctx_out)
        else:
            yx = mixer_pool(hx, pool_w_in[j], pool_w_grp[j], pool_scale[j], pool_w_out[j])
            yc = mixer_pool(hc, pool_w_in[j], pool_w_grp[j], pool_scale[j], pool_w_out[j]) if ctx_out else None
        xs = layer_norm(alpha * xs + g1 * yx, ln_g[i, 0], ln_b[i, 0])
        hx = modulate(xs, sh2, sc2)
        if ctx_out:
            cs = layer_norm(alpha * cs + cg1 * yc, ln_g[i, 0], ln_b[i, 0])
            hc = modulate(cs, csh2, csc2)
            f = peer(jnp.concatenate([hc, hx], axis=1).reshape(-1, d),
                     peer_wq[i], peer_keys[i], peer_u[i], peer_v[i]).reshape(bsz, c_len + s_len, d)
            fc, fx = f[:, :c_len], f[:, c_len:]
            cs = layer_norm(alpha * cs + cg2 * fc, ln_g[i, 1], ln_b[i, 1])
        else:
            fx = peer(hx.reshape(-1, d), peer_wq[i], peer_keys[i], peer_u[i], peer_v[i]).reshape(bsz, s_len, d)
        xs = layer_norm(alpha * xs + g2 * fx, ln_g[i, 1], ln_b[i, 1])
    return xs
```

```python
import math
import numpy as np
import concourse.bass as bass
import concourse.mybir as mybir
from concourse.bass_utils import run_bass_kernel_spmd
from contextlib import ExitStack

F32 = mybir.dt.float32
BF16 = mybir.dt.bfloat16
U32 = mybir.dt.uint32
AF = mybir.ActivationFunctionType
ALU = mybir.AluOpType
AX = mybir.AxisListType

D = 1024
SEQ = 8192
NQ = 2048
CTX = 256
DEPTH = 4
ALPHA = (2.0 * DEPTH) ** 0.25
EPS = 1e-5
NKEY = SEQ + CTX
NKC = NKEY // 128


class Buf:
    __slots__ = ("name", "w", "r")

    def __init__(self, name=""):
        self.name = name
        self.w = {}
        self.r = {}


class Prog:
    CE = ["pe", "act", "dve", "pool"]
    ENG = ["pe", "act", "dve", "pool", "sp"]

    def __init__(self, nc, stack, ring_sizes=None, same_engine_sync=True):
        self.nc = nc
        self.q = {e: [] for e in self.ENG}
        self.sems = []
        self.csem = {}
        for e in self.CE:
            self.csem[e] = self._newsem(stack, "c_" + e)
        self.cnt = {e: 0 for e in self.CE}
        self.waited = {e: {} for e in self.ENG}
        ring_sizes = ring_sizes or {"sp": 32, "pool": 48, "act": 16}
        self.ring = {}
        self.ring_val = {}
        self.ring_pos = {}
        for qn, n in ring_sizes.items():
            self.ring[qn] = [self._newsem(stack, f"d_{qn}{i}") for i in range(n)]
            self.ring_val[qn] = [0] * n
            self.ring_pos[qn] = 0
        self.same_engine_sync = same_engine_sync
        self.sb_off = 16640
        self.sb_max = 0
        self.sb_cap = 229376 - 128
        self.nalloc = 0

    def _newsem(self, stack, name):
        h = stack.enter_context(self.nc.semaphore(name))
        self.sems.append(h)
        return len(self.sems) - 1

    def sb(self, shape, dtype, name=None):
        esz = 2 if dtype == BF16 else 4
        per_part = int(np.prod(shape[1:])) * esz
        off = (self.sb_off + 63) // 64 * 64
        assert off + per_part <= self.sb_cap, f"SBUF overflow {off}+{per_part}"
        self.nalloc += 1
        t = self.nc.alloc_sbuf_tensor_at(f"t{self.nalloc}_{name or ''}", list(shape), dtype, offset=off)
        self.sb_off = off + per_part
        self.sb_max = max(self.sb_max, self.sb_off)
        return t

    def mark(self):
        return self.sb_off

    def release(self, mark):
        self.sb_off = mark

    def _deps(self, eng, reads, writes):
        toks = {}
        for b in reads:
            for s, v in b.w.items():
                if toks.get(s, 0) < v:
                    toks[s] = v
        for b in writes:
            for s, v in b.w.items():
                if toks.get(s, 0) < v:
                    toks[s] = v
            for s, v in b.r.items():
                if toks.get(s, 0) < v:
                    toks[s] = v
        out = []
        wd = self.waited[eng]
        own = self.csem.get(eng)
        for s, v in toks.items():
            if s == own and (eng == "pe" or not self.same_engine_sync):
                continue
            if wd.get(s, 0) >= v:
                continue
            wd[s] = v
            out.append((s, v))
        return out

    def _mark(self, tok, reads, writes):
        s, v = tok
        for b in reads:
            if b.r.get(s, 0) < v:
                b.r[s] = v
        for b in writes:
            b.w[s] = v
            b.r = {}

    def op(self, eng, fn, reads=(), writes=()):
        waits = self._deps(eng, reads, writes)
        self.cnt[eng] += 1
        s = self.csem[eng]
        self._mark((s, self.cnt[eng]), reads, writes)
        self.q[eng].append((waits, fn, s, 1))

    def dma(self, qn, fn, reads=(), writes=()):
        waits = self._deps(qn, reads, writes)
        n = len(self.ring[qn])
        i = self.ring_pos[qn] % n
        self.ring_pos[qn] += 1
        s = self.ring[qn][i]
        pv = self.ring_val[qn][i]
        if pv > 0 and self.waited[qn].get(s, 0) < pv:
            self.waited[qn][s] = pv
            waits.append((s, pv))
        self.ring_val[qn][i] = pv + 16
        tok = (s, pv + 16)
        self._mark(tok, reads, writes)
        self.q[qn].append((waits, fn, s, 16))
        return tok

    def barrier(self):
        toks = []
        for e in self.CE:
            if self.cnt[e] > 0:
                toks.append((self.csem[e], self.cnt[e]))
        for qn in self.ring:
            for i, s in enumerate(self.ring[qn]):
                if self.ring_val[qn][i] > 0:
                    toks.append((s, self.ring_val[qn][i]))
        for e in self.ENG:
            own = self.csem.get(e)
            ws = []
            for s, v in toks:
                if s == own:
                    continue
                if self.waited[e].get(s, 0) >= v:
                    continue
                self.waited[e][s] = v
                ws.append((s, v))
            if ws:
                self.q[e].append((ws, None, None, 0))

    def emit(self):
        nc = self.nc
        sems = self.sems

        def replay(name, eng):
            for waits, fn, s, inc in self.q[name]:
                for ws, wv in waits:
                    eng.wait_ge(sems[ws], wv)
                if fn is not None:
                    fn().then_inc(sems[s], inc)

        with nc.Block() as block:
            @block.tensor
            def _(e):
                replay("pe", e)

            @block.scalar
            def _(e):
                replay("act", e)

            @block.vector
            def _(e):
                replay("dve", e)

            @block.gpsimd
            def _(e):
                replay("pool", e)

            @block.sync
            def _(e):
                replay("sp", e)


class Tl:
    def __init__(self, P, shape, dtype, name=""):
        self.t = P.sb(shape, dtype, name)
        self.b = Buf(name)

    def __getitem__(self, k):
        return self.t[k]


class LayerProg:
    def __init__(self, li, dbg=False):
        self.li = li
        self.even = li % 2 == 0
        self.ctx_out = li < 2
        self.has_hc = li <= 2
        self.lam_init = 0.8 - 0.6 * math.exp(-0.3 * li)
        self.dbg = dbg
        self.nc = bass.Bass("TRN2", target_bir_lowering=False)
        self.dbufs = {}
        self.in_names = []
        self.out_names = []

    def din(self, name, shape, dtype=F32):
        self.in_names.append(name)
        self.dbufs[name] = Buf(name)
        return self.nc.dram_tensor(name, list(shape), dtype, kind="ExternalInput").ap()

    def dout(self, name, shape, dtype=F32):
        self.out_names.append(name)
        self.dbufs[name] = Buf(name)
        return self.nc.dram_tensor(name, list(shape), dtype, kind="ExternalOutput").ap()

    def dscr(self, name, shape, dtype=F32):
        self.dbufs[name] = Buf(name)
        if self.dbg and dtype == F32:
            self.out_names.append(name)
            return self.nc.dram_tensor(name, list(shape), dtype, kind="ExternalOutput").ap()
        return self.nc.dram_tensor(name, list(shape), dtype).ap()

    def V(self, m, r, w, *a, **k):
        f = getattr(self.nc.vector, m)
        self.P.op("dve", lambda: f(*a, **k), r, w)

    def A(self, m, r, w, *a, **k):
        f = getattr(self.nc.scalar, m)
        self.P.op("act", lambda: f(*a, **k), r, w)

    def G(self, m, r, w, *a, **k):
        f = getattr(self.nc.gpsimd, m)
        self.P.op("pool", lambda: f(*a, **k), r, w)

    def M(self, m, r, w, *a, **k):
        f = getattr(self.nc.tensor, m)
        self.P.op("pe", lambda: f(*a, **k), r, w)

    def DMA(self, q, r, w, out, in_):
        eng = {"sp": self.nc.sync, "pool": self.nc.gpsimd, "act": self.nc.scalar}[q]
        self.P.dma(q, lambda: eng.dma_start(out=out, in_=in_), r, w)

    def T(self, shape, dtype, name=""):
        return Tl(self.P, shape, dtype, name)

    def build(self):
        nc = self.nc
        li = self.li
        self.xs_in = self.din("xs_in", [NQ, D])
        self.xs_out = self.dout("xs_out", [NQ, D])
        if self.has_hc:
            self.cs_in = self.din("cs_in", [CTX, D])
        if self.ctx_out:
            self.cs_out = self.dout("cs_out", [CTX, D])
        self.c_in = self.din("c_in", [8, 128])
        self.cc_in = self.din("cc_in", [8, 128])
        self.ada_w = self.din("ada_w", [D, 6 * D])
        self.ada_b = self.din("ada_b", [6 * D])
        self.ln_g = self.din("ln_g", [2, D])
        self.ln_b = self.din("ln_b", [2, D])
        self.wq_d = self.din("peer_wq", [D, 2048])
        self.keys_d = self.din("peer_keys", [16, 128, 128])
        self.u_d = self.din("peer_u", [16384, D])
        self.v_d = self.din("peer_v", [16384, D])
        if self.even:
            self.xs_full = self.din("xs_full", [SEQ, D])
            self.w_in = self.din("ab_w_in", [D, 4096])
            self.w_out = self.din("ab_w_out", [1536, D])
            self.dlam = self.din("diff_lam", [256])
            self.dng = self.din("diff_norm_g", [128])
            self.sgg = self.din("sgu_ln_g", [512])
            self.sgb = self.din("sgu_ln_b", [512])
            self.sgw = self.din("sgu_w", [4, 128, 128])
            self.sgbias = self.din("sgu_b", [4, 128])
            self.cosk = self.din("cosk", [128, SEQ])
            self.sink = self.din("sink", [128, SEQ])
            self.cosq = self.din("cosq", [128, NQ])
            self.sinq = self.din("sinq", [128, NQ])
            self.KTd = self.dscr("KTd", [8, 128, NKEY], BF16)
            self.Vd = self.dscr("Vd", [8, NKEY, 129], BF16)
            self.QTd = self.dscr("QTd", [8, 128, NQ + CTX], BF16)
            self.Ad = self.dscr("Ad", [NQ + CTX, 1536])
        else:
            self.halo = self.din("halo", [64, D])
            self.halo_valid = self.din("halo_valid", [64])
            self.fix_x = self.din("fix_x", [64])
            self.fix_c = self.din("fix_c", [64])
            self.pw_in = self.din("pool_w_in", [D, D])
            self.pw_grp = self.din("pool_w_grp", [4, 256, 256])
            self.p_scale = self.din("pool_scale", [8, 128])
            self.pw_out = self.din("pool_w_out", [D, D])
        self.xs1d = self.dscr("xs1d", [NQ + CTX, D])
        if self.dbg and not self.even:
            self.dbg_d = self.dout("dbg_d", [8, 128, 512])
            self.dbg_m2 = self.dout("dbg_m2", [8, 128, 512])

        with ExitStack() as st:
            self.P = P = Prog(nc, st)
            self.ps = nc.alloc_psum_tensor("ps", [128, 8, 512], F32)
            self.pb = [Buf(f"bank{k}") for k in range(8)]
            import os
            stop = int(os.environ.get("KSTOP", "99"))
            self.setup()
            P.barrier()
            base = P.mark()
            if self.even:
                if stop >= 1:
                    self.kv_phase()
                    P.barrier()
                    P.release(base)
                if stop >= 2:
                    self.qsgu_phase()
                    P.barrier()
                    P.release(base)
                if stop >= 3:
                    self.attn_phase()
                    P.barrier()
                    P.release(base)
                if stop >= 4:
                    self.outproj_phase()
            else:
                if stop >= 1:
                    self.pool_phase()
            P.barrier()
            P.release(base)
            if stop >= 5:
                self.peer_phase()
            P.barrier()
            P.emit()
        return nc

    def setup(self):
        nc, P = self.nc, self.P
        ps, pb = self.ps, self.pb
        d = self.dbufs
        self.ident = ident = self.T([128, 128], F32, "ident")
        self.G("memset", [], [ident.b], ident[:], 0.0)
        self.G("affine_select", [ident.b], [ident.b], out=ident[:], in_=ident[:], pattern=[[-1, 128]],
               compare_op=ALU.not_equal, fill=1.0, base=0, channel_multiplier=1)
        self.iota16 = self.T([128, 16], F32, "iota16")
        self.G("iota", [], [self.iota16.b], self.iota16[:], pattern=[[1, 16]], base=0, channel_multiplier=0,
               allow_small_or_imprecise_dtypes=True)
        self.epsc = self.T([128, 1], F32, "eps")
        self.V("memset", [], [self.epsc.b], self.epsc[:], EPS)
        self.bnst = self.T([128, 12], F32, "bnst")
        self.lnr = self.T([128, D], F32, "lnr")
        self.lnmv = self.T([128, 4], F32, "lnmv")
        self.lng = [self.T([128, D], F32, f"lng{k}") for k in range(2)]
        self.lnb = [self.T([128, D], F32, f"lnb{k}") for k in range(2)]
        for k in range(2):
            self.DMA("sp", [d["ln_g"]], [self.lng[k].b], self.lng[k][:], self.ln_g[k].partition_broadcast(128))
            self.DMA("sp", [d["ln_b"]], [self.lnb[k].b], self.lnb[k][:], self.ln_b[k].partition_broadcast(128))
        streams = [("x", self.c_in, "c_in")]
        if self.has_hc:
            streams.append(("c", self.cc_in, "cc_in"))
        self.mod = {}
        self.col = {}
        scb = {}
        for sn, cin, cname in streams:
            self.mod[sn] = self.T([128, 6 * D], F32, "mod" + sn)
            self.DMA("sp", [d["ada_b"]], [self.mod[sn].b], self.mod[sn][:], self.ada_b.partition_broadcast(128))
            self.col[sn] = self.T([128, 2, 8], F32, "col" + sn)
        m0 = P.mark()
        for sn, cin, cname in streams:
            ct = self.T([8, 128], F32, "ct")
            self.DMA("sp", [d[cname]], [ct.b], ct[:], cin[:, :])
            self.M("transpose", [ct.b, ident.b], [pb[0]], out=ps[:, 0, 0:8], in_=ct[:], identity=ident[0:8, 0:8])
            scT = self.T([128, 8], F32, "scT")
            self.A("activation", [pb[0]], [scT.b], out=scT[:], in_=ps[:, 0, 0:8], func=AF.Silu)
            scb[sn] = self.T([128, 8, 128], F32, "scb")
            for c in range(8):
                self.V("tensor_copy", [scT.b], [scb[sn].b], out=scb[sn][:, c, :],
                       in_=scT[:, c:c + 1].to_broadcast([128, 128]))
        nbs = list(range(12))
        wring = [self.T([128, 8, 512], F32, f"adaw{k}") for k in range(2)]
        adaw_v = self.ada_w.rearrange("(c p) n -> p c n", p=128)
        for it, nb in enumerate(nbs):
            wt = wring[it % 2]
            self.DMA("sp", [d["ada_w"]], [wt.b], wt[:], adaw_v[:, :, nb * 512:(nb + 1) * 512])
            for si, (sn, _, _) in enumerate(streams):
                if sn == "c" and not self.ctx_out and nb >= 4:
                    continue
                bk = 1 + si
                for c in range(8):
                    self.M("matmul", [scb[sn].b, wt.b], [pb[bk]], ps[:, bk, :], lhsT=scb[sn][:, c, :],
                           rhs=wt[:, c, :], start=(c == 0), stop=(c == 7))
                msl = self.mod[sn][:, nb * 512:(nb + 1) * 512]
                self.V("tensor_tensor", [pb[bk], self.mod[sn].b], [self.mod[sn].b], out=msl, in0=ps[:, bk, :],
                       in1=msl, op=ALU.add)
        for sn, _, _ in streams:
            md = self.mod[sn]
            self.V("tensor_scalar_add", [md.b], [md.b], out=md[:, D:2 * D], in0=md[:, D:2 * D], scalar1=1.0)
            if sn == "x" or self.ctx_out:
                self.V("tensor_scalar_add", [md.b], [md.b], out=md[:, 4 * D:5 * D], in0=md[:, 4 * D:5 * D],
                       scalar1=1.0)
            for k in range(2):
                for c in range(8):
                    o = k * D + c * 128
                    self.M("transpose", [md.b, ident.b], [pb[3]], out=ps[:, 3, 0:32], in_=md[0:32, o:o + 128],
                           identity=ident[0:32, 0:32])
                    self.V("tensor_copy", [pb[3]], [self.col[sn].b], out=self.col[sn][:, k, c:c + 1],
                           in_=ps[:, 3, 0:1])
        P.barrier()
        P.release(m0)

    def load_w_bf16(self, dst, src_ap, dname, nchunks):
        for c in range(nchunks):
            self.DMA("pool", [self.dbufs[dname]], [dst.b], dst[:, c, :], src_ap[c * 128:(c + 1) * 128, :])

    def make_hxT(self, hxT, col0, rows_ap, rows_buf, ncols_tok, colset, tile_ring, it, nrows=128, dt_scale=None):
        ps, pb = self.ps, self.pb
        xt = tile_ring[it % len(tile_ring)]
        self.DMA("sp", [rows_buf], [xt.b], xt[0:nrows, :], rows_ap)
        self.transpose_mod(hxT, col0, xt, 0, nrows, colset)

    def transpose_mod(self, hxT, col0, xt, p0, nrows, colset, banks=(0, 1)):
        ps, pb = self.ps, self.pb
        for c in range(8):
            bk = banks[c // 4]
            o = (c % 4) * 128
            self.M("transpose", [xt.b, self.ident.b], [pb[bk]], out=ps[:, bk, o:o + nrows],
                   in_=xt[p0:p0 + nrows, c * 128:(c + 1) * 128], identity=self.ident[p0:p0 + nrows, p0:p0 + nrows])
        for c in range(8):
            bk = banks[c // 4]
            o = (c % 4) * 128
            if colset is None:
                self.A("copy", [pb[bk]], [hxT.b], out=hxT[:, c, col0:col0 + nrows], in_=ps[:, bk, o:o + nrows])
            else:
                self.A("activation", [pb[bk], colset.b], [hxT.b], out=hxT[:, c, col0:col0 + nrows],
                       in_=ps[:, bk, o:o + nrows], func=AF.Identity, bias=colset[:, 0, c:c + 1],
                       scale=colset[:, 1, c:c + 1])

    def make_rot(self, w, wr):
        for c in range(8):
            wv = w[:, c, :].rearrange("p (g t d) -> p g t d", t=2, d=32)
            wrv = wr[:, c, :].rearrange("p (g t d) -> p g t d", t=2, d=32)
            self.V("tensor_scalar_mul", [w.b], [wr.b], out=wrv[:, :, 0, :], in0=wv[:, :, 1, :], scalar1=-1.0)
            self.G("tensor_copy", [w.b], [wr.b], out=wrv[:, :, 1, :], in_=wv[:, :, 0, :])

    def kv_phase(self):
        nc, P, ps, pb, d = self.nc, self.P, self.ps, self.pb, self.dbufs
        wk = self.T([128, 8, 1024], BF16, "wk")
        wkr = self.T([128, 8, 1024], BF16, "wkr")
        wv = self.T([128, 8, 1024], BF16, "wv")
        self.load_w_bf16(wk, self.w_in[:, 1024:2048], "ab_w_in", 8)
        self.load_w_bf16(wv, self.w_in[:, 2048:3072], "ab_w_in", 8)
        self.make_rot(wk, wkr)
        xring = [self.T([128, D], F32, f"xr{k}") for k in range(3)]
        hring = [self.T([128, 8, 512], BF16, f"hxT{k}") for k in range(2)]
        cring = [self.T([128, 512], F32, f"cos{k}") for k in range(2)]
        sring = [self.T([128, 512], F32, f"sin{k}") for k in range(2)]
        t1r = [self.T([128, 512], F32, f"t1{k}") for k in range(2)]
        t2r = [self.T([128, 512], F32, f"t2{k}") for k in range(2)]
        ktr = [self.T([128, 512], BF16, f"kt{k}") for k in range(3)]
        vtr = [self.T([128, 8, 129], BF16, f"vt{k}") for k in range(2)]
        for v_ in vtr:
            self.V("memset", [], [v_.b], v_[:], 1.0)
        blocks = [("c", 0, 256)] + [("x", b * 512, 512) for b in range(16)]
        xi = 0
        ki = 0
        vi = 0
        for bi, (kind, t0, n) in enumerate(blocks):
            hxT = hring[bi % 2]
            ntile = n // 128
            for tt in range(ntile):
                if kind == "c":
                    src, sb_ = self.cs_in[tt * 128:(tt + 1) * 128, :], d["cs_in"]
                else:
                    src, sb_ = self.xs_full[t0 + tt * 128:t0 + (tt + 1) * 128, :], d["xs_full"]
                self.make_hxT(hxT, tt * 128, src, sb_, n, self.col[kind if kind == "x" else "c"], xring, xi)
                xi += 1
            key0 = 0 if kind == "c" else CTX + t0
            if kind == "x":
                ct, stl = cring[bi % 2], sring[bi % 2]
                self.DMA("sp", [d["cosk"]], [ct.b], ct[:], self.cosk[:, t0:t0 + 512])
                self.DMA("sp", [d["sink"]], [stl.b], stl[:], self.sink[:, t0:t0 + 512])
            for h in range(8):
                ba, bb = (2, 3) if h % 2 == 0 else (4, 5)
                for c in range(8):
                    self.M("matmul", [wk.b, hxT.b], [pb[ba]], ps[:, ba, 0:n], lhsT=wk[:, c, h * 128:(h + 1) * 128],
                           rhs=hxT[:, c, 0:n], start=(c == 0), stop=(c == 7))
                kt = ktr[ki % 3]
                ki += 1
                if kind == "x":
                    for c in range(8):
                        self.M("matmul", [wkr.b, hxT.b], [pb[bb]], ps[:, bb, 0:n],
                               lhsT=wkr[:, c, h * 128:(h + 1) * 128], rhs=hxT[:, c, 0:n], start=(c == 0),
                               stop=(c == 7))
                    t1, t2 = t1r[h % 2], t2r[h % 2]
                    self.V("tensor_tensor", [pb[ba], ct.b], [t1.b], out=t1[:], in0=ps[:, ba, :], in1=ct[:],
                           op=ALU.mult)
                    self.V("tensor_tensor", [pb[bb], stl.b], [t2.b], out=t2[:], in0=ps[:, bb, :], in1=stl[:],
                           op=ALU.mult)
                    self.G("tensor_tensor", [t1.b, t2.b], [kt.b], out=kt[:], in0=t1[:], in1=t2[:], op=ALU.add)
                else:
                    self.A("copy", [pb[ba]], [kt.b], out=kt[:, 0:n], in_=ps[:, ba, 0:n])
                self.DMA("sp", [kt.b], [d["KTd"]], self.KTd[h, :, key0:key0 + n], kt[:, 0:n])
            for tt in range(ntile):
                vt = vtr[vi % 2]
                vi += 1
                for nb in range(2):
                    bk = 6 + nb
                    for c in range(8):
                        self.M("matmul", [hxT.b, wv.b], [pb[bk]], ps[:, bk, :], lhsT=hxT[:, c, tt * 128:(tt + 1) * 128],
                               rhs=wv[:, c, nb * 512:(nb + 1) * 512], start=(c == 0), stop=(c == 7))
                    self.A("copy", [pb[bk]], [vt.b], out=vt[:, nb * 4:(nb + 1) * 4, 0:128],
                           in_=ps[:, bk, :].rearrange("p (h e) -> p h e", e=128))
                k0 = key0 + tt * 128
                self.DMA("sp", [vt.b], [d["Vd"]], self.Vd[:, k0:k0 + 128, :].rearrange("h p e -> p h e"), vt[:])

    def gelu_ln_stats(self, src_t, nfree, mv):
        st = self.bnst
        nk = nfree // 512
        for k in range(nk):
            self.V("bn_stats", [src_t.b], [st.b], out=st[:, k * 6:(k + 1) * 6], in_=src_t[:, k * 512:(k + 1) * 512])
        self.V("bn_aggr", [st.b], [mv.b], out=mv[:, 0:2], in_=st[:, 0:nk * 6])

    def rstd_from(self, var_ap, var_buf, out_t, scale=1.0):
        self.A("activation", [var_buf, self.epsc.b], [out_t.b], out=out_t[:, 0:1], in_=var_ap, func=AF.Sqrt,
               bias=self.epsc[:, 0:1], scale=scale)
        self.V("reciprocal", [out_t.b], [out_t.b], out=out_t[:, 0:1], in_=out_t[:, 0:1])

    def qsgu_phase(self):
        nc, P, ps, pb, d = self.nc, self.P, self.ps, self.pb, self.dbufs
        wq = self.T([128, 8, 1024], BF16, "wq")
        wqr = self.T([128, 8, 1024], BF16, "wqr")
        wsg = self.T([128, 8, 1024], BF16, "wsg")
        self.load_w_bf16(wq, self.w_in[:, 0:1024], "ab_w_in", 8)
        self.load_w_bf16(wsg, self.w_in[:, 3072:4096], "ab_w_in", 8)
        self.make_rot(wq, wqr)
        wsT = self.T([128, 4, 128], BF16, "wsT")
        bsT = self.T([128, 4], F32, "bsT")
        m0 = P.mark()
        wtmp = self.T([128, 4, 128], F32, "wtmp")
        self.DMA("sp", [d["sgu_w"]], [wtmp.b], wtmp[:], self.sgw.rearrange("g p q -> p g q"))
        for g in range(4):
            self.M("transpose", [wtmp.b, self.ident.b], [pb[0]], out=ps[:, 0, g * 128:(g + 1) * 128], in_=wtmp[:, g, :],
                   identity=self.ident[:])
        self.A("copy", [pb[0]], [wsT.b], out=wsT[:], in_=ps[:, 0, :].rearrange("p (g q) -> p g q", q=128))
        btmp = self.T([4, 128], F32, "btmp")
        self.DMA("sp", [d["sgu_b"]], [btmp.b], btmp[:], self.sgbias[:, :])
        self.M("transpose", [btmp.b, self.ident.b], [pb[1]], out=ps[:, 1, 0:4], in_=btmp[:], identity=self.ident[0:4, 0:4])
        self.V("tensor_copy", [pb[1]], [bsT.b], out=bsT[:], in_=ps[:, 1, 0:4])
        sg_g = self.T([128, 512], F32, "sg_g")
        sg_b = self.T([128, 512], F32, "sg_b")
        self.DMA("sp", [d["sgu_ln_g"]], [sg_g.b], sg_g[:], self.sgg.partition_broadcast(128))
        self.DMA("sp", [d["sgu_ln_b"]], [sg_b.b], sg_b[:], self.sgb.partition_broadcast(128))
        xring = [self.T([128, D], F32, f"xr{k}") for k in range(3)]
        hring = [self.T([128, 8, 512], BF16, f"hxT{k}") for k in range(2)]
        cring = [self.T([128, 512], F32, f"cos{k}") for k in range(2)]
        sring = [self.T([128, 512], F32, f"sin{k}") for k in range(2)]
        t1r = [self.T([128, 512], F32, f"t1{k}") for k in range(2)]
        t2r = [self.T([128, 512], F32, f"t2{k}") for k in range(2)]
        qtr = [self.T([128, 512], BF16, f"qt{k}") for k in range(3)]
        ur = [self.T([128, 512], F32, f"u{k}") for k in range(2)]
        gvr = [self.T([128, 512], F32, f"gv{k}") for k in range(2)]
        vbr = [self.T([128, 512], BF16, f"vb{k}") for k in range(2)]
        gxr = [self.T([128, 512], F32, f"gx{k}") for k in range(2)]
        mvr = [self.T([128, 4], F32, f"mv{k}") for k in range(2)]
        blocks = [("x", b * 512, 512) for b in range(4)]
        if self.ctx_out:
            blocks.append(("c", 0, 256))
        xi = 0
        qi = 0
        si = 0
        for bi, (kind, t0, n) in enumerate(blocks):
            hxT = hring[bi % 2]
            ntile = n // 128
            for tt in range(ntile):
                if kind == "c":
                    src, sb_ = self.cs_in[tt * 128:(tt + 1) * 128, :], d["cs_in"]
                else:
                    src, sb_ = self.xs_in[t0 + tt * 128:t0 + (tt + 1) * 128, :], d["xs_in"]
                self.make_hxT(hxT, tt * 128, src, sb_, n, self.col[kind], xring, xi)
                xi += 1
            q0 = t0 if kind == "x" else NQ
            if kind == "x":
                ct, stl = cring[bi % 2], sring[bi % 2]
                self.DMA("sp", [d["cosq"]], [ct.b], ct[:], self.cosq[:, t0:t0 + 512])
                self.DMA("sp", [d["sinq"]], [stl.b], stl[:], self.sinq[:, t0:t0 + 512])
            for h in range(8):
                ba, bb = (2, 3) if h % 2 == 0 else (4, 5)
                for c in range(8):
                    self.M("matmul", [wq.b, hxT.b], [pb[ba]], ps[:, ba, 0:n], lhsT=wq[:, c, h * 128:(h + 1) * 128],
                           rhs=hxT[:, c, 0:n], start=(c == 0), stop=(c == 7))
                qt = qtr[qi % 3]
                qi += 1
                if kind == "x":
                    for c in range(8):
                        self.M("matmul", [wqr.b, hxT.b], [pb[bb]], ps[:, bb, 0:n],
                               lhsT=wqr[:, c, h * 128:(h + 1) * 128], rhs=hxT[:, c, 0:n], start=(c == 0),
                               stop=(c == 7))
                    t1, t2 = t1r[h % 2], t2r[h % 2]
                    self.V("tensor_tensor", [pb[ba], ct.b], [t1.b], out=t1[:], in0=ps[:, ba, :], in1=ct[:],
                           op=ALU.mult)
                    self.V("tensor_tensor", [pb[bb], stl.b], [t2.b], out=t2[:], in0=ps[:, bb, :], in1=stl[:],
                           op=ALU.mult)
                    self.G("tensor_tensor", [t1.b, t2.b], [qt.b], out=qt[:], in0=t1[:], in1=t2[:], op=ALU.add)
                else:
                    self.A("copy", [pb[ba]], [qt.b], out=qt[:, 0:n], in_=ps[:, ba, 0:n])
                self.DMA("sp", [qt.b], [d["QTd"]], self.QTd[h, :, q0:q0 + n], qt[:, 0:n])
            for tt in range(ntile):
                u, gv, vb, gx, mv = ur[si % 2], gvr[si % 2], vbr[si % 2], gxr[si % 2], mvr[si % 2]
                si += 1
                for nb in range(2):
                    bk = 6 + nb
                    for c in range(8):
                        self.M("matmul", [hxT.b, wsg.b], [pb[bk]], ps[:, bk, :], lhsT=hxT[:, c, tt * 128:(tt + 1) * 128],
                               rhs=wsg[:, c, nb * 512:(nb + 1) * 512], start=(c == 0), stop=(c == 7))
                self.A("activation", [pb[6]], [u.b], out=u[:], in_=ps[:, 6, :], func=AF.Gelu_apprx_tanh)
                self.A("activation", [pb[7]], [gv.b], out=gv[:], in_=ps[:, 7, :], func=AF.Gelu_apprx_tanh)
                self.gelu_ln_stats(gv, 512, mv)
                self.rstd_from(mv[:, 1:2], mv.b, Tlv(mv, 2))
                self.V("tensor_scalar", [gv.b, mv.b], [gv.b], out=gv[:], in0=gv[:], scalar1=mv[:, 0:1],
                       scalar2=mv[:, 2:3], op0=ALU.subtract, op1=ALU.mult)
                self.V("tensor_tensor", [gv.b, sg_g.b], [gv.b], out=gv[:], in0=gv[:], in1=sg_g[:], op=ALU.mult)
                self.G("tensor_tensor", [gv.b, sg_b.b], [vb.b], out=vb[:], in0=gv[:], in1=sg_b[:], op=ALU.add)
                for g in range(4):
                    self.M("matmul", [wsT.b, vb.b], [pb[1]], ps[:, 1, g * 128:(g + 1) * 128], lhsT=wsT[:, g, :],
                           rhs=vb[:, g * 128:(g + 1) * 128], start=True, stop=True)
                for g in range(4):
                    sl = slice(g * 128, (g + 1) * 128)
                    self.V("scalar_tensor_tensor", [pb[1], bsT.b, u.b], [gx.b], out=gx[:, sl], in0=ps[:, 1, sl],
                           scalar=bsT[:, g:g + 1], in1=u[:, sl], op0=ALU.add, op1=ALU.mult)
                r0 = q0 + tt * 128
                self.DMA("sp", [gx.b], [d["Ad"]], self.Ad[r0:r0 + 128, 1024:1536], gx[:])

    def attn_phase(self):
        nc, P, ps, pb, d = self.nc, self.P, self.ps, self.pb, self.dbufs
        NQT = NQ + (CTX if self.ctx_out else 0)
        dl = self.T([128, 256], F32, "dl")
        self.DMA("sp", [d["diff_lam"]], [dl.b], dl[:], self.dlam.partition_broadcast(128))
        lam = self.T([128, 4], F32, "lam")
        junk = self.T([128, 128], F32, "junk")
        self.V("scalar_tensor_tensor", [dl.b], [junk.b, lam.b], out=junk[:, 0:64], in0=dl[:, 0:64], scalar=1.0,
               in1=dl[:, 64:128], op0=ALU.mult, op1=ALU.mult, accum_out=lam[:, 0:1])
        self.V("scalar_tensor_tensor", [dl.b], [junk.b, lam.b], out=junk[:, 0:64], in0=dl[:, 128:192], scalar=1.0,
               in1=dl[:, 192:256], op0=ALU.mult, op1=ALU.mult, accum_out=lam[:, 1:2])
        self.A("activation", [lam.b], [lam.b], out=lam[:, 0:2], in_=lam[:, 0:2], func=AF.Exp)
        self.V("tensor_tensor", [lam.b], [lam.b], out=lam[:, 2:3], in0=lam[:, 1:2], in1=lam[:, 0:1], op=ALU.subtract)
        self.V("tensor_scalar_add", [lam.b], [lam.b], out=lam[:, 2:3], in0=lam[:, 2:3], scalar1=-self.lam_init)
        gsc = self.T([128, 128], F32, "gsc")
        self.DMA("sp", [d["diff_norm_g"]], [gsc.b], gsc[:], self.dng.partition_broadcast(128))
        self.V("tensor_scalar_mul", [gsc.b], [gsc.b], out=gsc[:], in0=gsc[:], scalar1=(1.0 - self.lam_init))
        ktr = [self.T([128, NKEY], BF16, f"KT{k}") for k in range(2)]
        vr = [self.T([128, NKC, 129], BF16, f"V{k}") for k in range(2)]
        qr = [self.T([128, NQT], BF16, f"QT{k}") for k in range(2)]
        ptr = [self.T([128, 512], BF16, f"PT{k}") for k in range(3)]
        o1r = [self.T([128, 128], F32, f"o1{k}") for k in range(2)]
        orr = [self.T([128, 128], F32, f"o{k}") for k in range(2)]
        ar = [self.T([128, 128], F32, f"a{k}") for k in range(3)]
        smr = [self.T([128, 4], F32, f"sm{k}") for k in range(2)]
        qblocks = [("x", b * 256, NKC) for b in range(NQ // 256)]
        if self.ctx_out:
            qblocks.append(("c", NQ, 2))
        it = 0
        pi = 0
        ei = 0
        for h in range(8):
            KT, Vh, QT = ktr[h % 2], vr[h % 2], qr[h % 2]
            self.DMA("sp", [d["KTd"]], [KT.b], KT[:], self.KTd[h, :, :])
            vview = self.Vd[h, :, :].rearrange("(c p) e -> p c e", p=128)
            for c0 in range(0, NKC, 11):
                self.DMA("sp", [d["Vd"]], [Vh.b], Vh[:, c0:c0 + 11, :], vview[:, c0:c0 + 11, :])
            self.DMA("sp", [d["QTd"]], [QT.b], QT[:], self.QTd[h, :, 0:NQT])
            for (kind, q0, nkc) in qblocks:
                for kc in range(nkc):
                    sbk = (0, 1) if it % 2 == 0 else (6, 7)
                    it += 1
                    for m in range(2):
                        self.M("matmul", [KT.b, QT.b], [pb[sbk[m]]], ps[:, sbk[m], 0:256],
                               lhsT=KT[m * 64:(m + 1) * 64, kc * 128:(kc + 1) * 128],
                               rhs=QT[m * 64:(m + 1) * 64, q0:q0 + 256], start=True, stop=True)
                    PT = ptr[pi % 3]
                    pi += 1
                    self.A("activation", [pb[sbk[0]], pb[sbk[1]]], [PT.b], out=PT[:].rearrange("p (m q) -> p m q", m=2),
                           in_=ps[:, sbk[0]:sbk[0] + 2, 0:256], func=AF.Exp, scale=0.125)
                    for m in range(2):
                        for qs in range(2):
                            bk = 2 + m * 2 + qs
                            self.M("matmul", [PT.b, Vh.b], [pb[bk]], ps[:, bk, 0:129],
                                   lhsT=PT[:, m * 256 + qs * 128:m * 256 + (qs + 1) * 128], rhs=Vh[:, kc, :],
                                   start=(kc == 0), stop=(kc == nkc - 1))
                for qs in range(2):
                    o1, o, sm = o1r[ei % 2], orr[ei % 2], smr[ei % 2]
                    a = ar[ei % 3]
                    ei += 1
                    b0, b1 = 2 + qs, 4 + qs
                    self.V("reciprocal", [pb[b0]], [sm.b], out=sm[:, 0:1], in_=ps[:, b0, 128:129])
                    self.V("reciprocal", [pb[b1]], [sm.b], out=sm[:, 1:2], in_=ps[:, b1, 128:129])
                    self.V("tensor_tensor", [sm.b, lam.b], [sm.b], out=sm[:, 1:2], in0=sm[:, 1:2], in1=lam[:, 2:3],
                           op=ALU.mult)
                    self.V("tensor_scalar_mul", [pb[b0], sm.b], [o1.b], out=o1[:], in0=ps[:, b0, 0:128],
                           scalar1=sm[:, 0:1])
                    self.V("scalar_tensor_tensor", [pb[b1], sm.b, o1.b], [o.b], out=o[:], in0=ps[:, b1, 0:128],
                           scalar=sm[:, 1:2], in1=o1[:], op0=ALU.mult, op1=ALU.add)
                    self.V("scalar_tensor_tensor", [o.b], [junk.b, sm.b], out=junk[:], in0=o[:], scalar=1.0, in1=o[:],
                           op0=ALU.mult, op1=ALU.mult, accum_out=sm[:, 2:3])
                    self.rstd_from(sm[:, 2:3], sm.b, Tlv(sm, 3), scale=1.0 / 128.0)
                    self.V("scalar_tensor_tensor", [o.b, sm.b, gsc.b], [a.b], out=a[:], in0=o[:], scalar=sm[:, 3:4],
                           in1=gsc[:], op0=ALU.mult, op1=ALU.mult)
                    r0 = q0 + qs * 128
                    self.DMA("sp", [a.b], [d["Ad"]], self.Ad[r0:r0 + 128, h * 128:(h + 1) * 128], a[:])

    def ln_residual(self, y_ap, y_bufs, xs_t, gate_ap, gate_buf, k, out_t):
        r = self.lnr
        mv = self.lnmv
        self.V("tensor_tensor", y_bufs + [gate_buf], [r.b], out=r[:], in0=y_ap, in1=gate_ap, op=ALU.mult)
        self.V("scalar_tensor_tensor", [xs_t.b, r.b], [r.b], out=r[:], in0=xs_t[:], scalar=ALPHA, in1=r[:],
               op0=ALU.mult, op1=ALU.add)
        self.gelu_ln_stats(r, D, mv)
        self.rstd_from(mv[:, 1:2], mv.b, Tlv(mv, 2))
        self.V("tensor_scalar", [r.b, mv.b], [r.b], out=r[:], in0=r[:], scalar1=mv[:, 0:1], scalar2=mv[:, 2:3],
               op0=ALU.subtract, op1=ALU.mult)
        self.G("tensor_tensor", [r.b, self.lng[k].b], [r.b], out=r[:], in0=r[:], in1=self.lng[k][:], op=ALU.mult)
        self.G("tensor_tensor", [r.b, self.lnb[k].b], [out_t.b], out=out_t[:], in0=r[:], in1=self.lnb[k][:],
               op=ALU.add)

    def tile_list(self):
        tl = [("x", t) for t in range(NQ // 128)]
        if self.ctx_out:
            tl += [("c", t) for t in range(CTX // 128)]
        return tl

    def outproj_phase(self):
        nc, P, ps, pb, d = self.nc, self.P, self.ps, self.pb, self.dbufs
        wo = self.T([128, 12, 1024], BF16, "wo")
        self.load_w_bf16(wo, self.w_out, "ab_w_out", 12)
        for idx, (kind, t) in enumerate(self.tile_list()):
            m0 = P.mark()
            row0 = t * 128 if kind == "x" else NQ + t * 128
            At = self.T([128, 1536], F32, "At")
            self.DMA("sp", [d["Ad"]], [At.b], At[:], self.Ad[row0:row0 + 128, :])
            xs_t = self.T([128, D], F32, "xs_t")
            if kind == "x":
                self.DMA("sp", [d["xs_in"]], [xs_t.b], xs_t[:], self.xs_in[t * 128:(t + 1) * 128, :])
            else:
                self.DMA("sp", [d["cs_in"]], [xs_t.b], xs_t[:], self.cs_in[t * 128:(t + 1) * 128, :])
            AT = self.T([128, 12, 128], BF16, "AT")
            for k in range(12):
                bk = k // 4
                self.M("transpose", [At.b, self.ident.b], [pb[bk]], out=ps[:, bk, (k % 4) * 128:(k % 4 + 1) * 128],
                       in_=At[:, k * 128:(k + 1) * 128], identity=self.ident[:])
            for bk in range(3):
                self.A("copy", [pb[bk]], [AT.b], out=AT[:, bk * 4:(bk + 1) * 4, :],
                       in_=ps[:, bk, :].rearrange("p (k q) -> p k q", q=128))
            for nb in range(2):
                bk = 4 + nb
                for k in range(12):
                    self.M("matmul", [AT.b, wo.b], [pb[bk]], ps[:, bk, :], lhsT=AT[:, k, :],
                           rhs=wo[:, k, nb * 512:(nb + 1) * 512], start=(k == 0), stop=(k == 11))
            out_t = self.T([128, D], F32, "xs1")
            md = self.mod[kind]
            self.ln_residual(ps[:, 4:6, :].rearrange("p a b -> p (a b)"), [pb[4], pb[5]], xs_t, md[:, 2 * D:3 * D],
                             md.b, 0, out_t)
            self.DMA("sp", [out_t.b], [d["xs1d"]], self.xs1d[row0:row0 + 128, :], out_t[:])
            P.barrier()
            P.release(m0)

    def pool_phase(self):
        nc, P, ps, pb, d = self.nc, self.P, self.ps, self.pb, self.dbufs
        wi = self.T([128, 8, 1024], BF16, "pwi")
        wo = self.T([128, 8, 1024], BF16, "pwo")
        wg = self.T([128, 4, 2, 256], BF16, "pwg")
        self.load_w_bf16(wi, self.pw_in, "pool_w_in", 8)
        self.load_w_bf16(wo, self.pw_out, "pool_w_out", 8)
        for g in range(4):
            for c in range(2):
                self.DMA("pool", [d["pool_w_grp"]], [wg.b], wg[:, g, c, :], self.pw_grp[g, c * 128:(c + 1) * 128, :])
        psc = self.T([128, 8], F32, "psc")
        m0 = P.mark()
        pt = self.T([8, 128], F32, "pt")
        self.DMA("sp", [d["pool_scale"]], [pt.b], pt[:], self.p_scale[:, :])
        self.M("transpose", [pt.b, self.ident.b], [pb[0]], out=ps[:, 0, 0:8], in_=pt[:], identity=self.ident[0:8, 0:8])
        self.V("tensor_copy", [pb[0]], [psc.b], out=psc[:], in_=ps[:, 0, 0:8])
        hv = self.T([128, 64], F32, "hv")
        self.DMA("sp", [d["halo_valid"]], [hv.b], hv[:], self.halo_valid.partition_broadcast(128))
        fx = self.T([128, 64], F32, "fx")
        self.DMA("sp", [d["fix_x"]], [fx.b], fx[:], self.fix_x.partition_broadcast(128))
        fc = self.T([128, 64], F32, "fc")
        self.DMA("sp", [d["fix_c"]], [fc.b], fc[:], self.fix_c.partition_broadcast(128))
        halo_t = self.T([64, D], F32, "halo_t")
        self.DMA("sp", [d["halo"]], [halo_t.b], halo_t[:], self.halo[:, :])
        base = P.mark()
        blocks = [("x", b) for b in range(4)]
        if self.ctx_out:
            blocks.append(("c", 0))
        for kind, b in blocks:
            n = 512 if kind == "x" else 256
            W = n + 64
            colset = self.col[kind]
            hxT = self.T([128, 8, W], BF16, "phxT")
            xts = []
            ntile = n // 128
            t0 = b * 4
            src = self.xs_in if kind == "x" else self.cs_in
            sname = "xs_in" if kind == "x" else "cs_in"
            xs_tiles = []
            for tt in range(ntile):
                xt = self.T([128, D], F32, f"pxt{tt}")
                self.DMA("sp", [d[sname]], [xt.b], xt[:], src[(t0 + tt) * 128:(t0 + tt + 1) * 128, :])
                xs_tiles.append(xt)
                self.transpose_mod(hxT, 32 + tt * 128, xt, 0, 128, colset, banks=(0, 1))
            if kind == "c":
                self.V("memset", [], [hxT.b], hxT[:, :, 0:32], 0.0)
                self.V("memset", [], [hxT.b], hxT[:, :, W - 32:W], 0.0)
            else:
                lt = self.T([128, D], F32, "plt")
                rt_ = self.T([128, D], F32, "prt")
                if b == 0:
                    self.transpose_mod(hxT, 0, halo_t, 0, 32, colset, banks=(0, 1))
                    self.V("tensor_tensor", [hxT.b, hv.b], [hxT.b], out=hxT[:, :, 0:32], in0=hxT[:, :, 0:32],
                           in1=hv[:, 0:32].unsqueeze(1).to_broadcast([128, 8, 32]), op=ALU.mult)
                else:
                    self.DMA("sp", [d[sname]], [lt.b], lt[64:96, :], src[t0 * 128 - 32:t0 * 128, :])
                    self.transpose_mod(hxT, 0, lt, 64, 32, colset, banks=(0, 1))
                if b == 3:
                    self.transpose_mod(hxT, W - 32, halo_t, 32, 32, colset, banks=(0, 1))
                    self.V("tensor_tensor", [hxT.b, hv.b], [hxT.b], out=hxT[:, :, W - 32:W], in0=hxT[:, :, W - 32:W],
                           in1=hv[:, 32:64].unsqueeze(1).to_broadcast([128, 8, 32]), op=ALU.mult)
                else:
                    self.DMA("sp", [d[sname]], [rt_.b], rt_[0:32, :], src[(t0 + 4) * 128:(t0 + 4) * 128 + 32, :])
                    self.transpose_mod(hxT, W - 32, rt_, 0, 32, colset, banks=(0, 1))
            dT = self.T([128, 8, n], BF16, "pdT")
            fixt = fx if kind == "x" else fc
            first = (kind == "c") or (b == 0)
            last = (kind == "c") or (b == 3)
            h2 = W // 2
            for cc in range(8):
                g = cc // 2
                w = 2 << g
                m1 = P.mark()
                mT = self.T([128, W], F32, "pmT")
                wa = self.T([128, W], F32, "pwa")
                wb = self.T([128, W], F32, "pwb")
                for half in range(2):
                    bk = 2 + half
                    for c in range(8):
                        self.M("matmul", [wi.b, hxT.b], [pb[bk]], ps[:, bk, 0:h2], lhsT=wi[:, c, cc * 128:(cc + 1) * 128],
                               rhs=hxT[:, c, half * h2:(half + 1) * h2], start=(c == 0), stop=(c == 7))
                    self.A("copy", [pb[bk]], [mT.b], out=mT[:, half * h2:(half + 1) * h2], in_=ps[:, bk, 0:h2])
                if g == 0:
                    self.V("tensor_tensor", [mT.b], [wa.b], out=wa[:, 32:32 + n], in0=mT[:, 31:31 + n],
                           in1=mT[:, 32:32 + n], op=ALU.add)
                    win = wa
                else:
                    self.V("tensor_tensor", [mT.b], [wa.b], out=wa[:, 0:W - 1], in0=mT[:, 0:W - 1], in1=mT[:, 1:W],
                           op=ALU.add)
                    cur, oth, span = wa, wb, 2
                    while span * 2 < w:
                        L = W - 2 * span + 1
                        self.V("tensor_tensor", [cur.b], [oth.b], out=oth[:, 0:L], in0=cur[:, 0:L],
                               in1=cur[:, span:span + L], op=ALU.add)
                        cur, oth = oth, cur
                        span *= 2
                    self.V("tensor_tensor", [cur.b], [oth.b], out=oth[:, 32:32 + n], in0=cur[:, 32 - span:32 - span + n],
                           in1=cur[:, 32:32 + n], op=ALU.add)
                    win = oth
                if first:
                    self.V("tensor_tensor", [win.b, fixt.b], [win.b], out=win[:, 32:40], in0=win[:, 32:40],
                           in1=fixt[:, g * 16:g * 16 + 8], op=ALU.mult)
                if last:
                    self.V("tensor_tensor", [win.b, fixt.b], [win.b], out=win[:, 24 + n:32 + n], in0=win[:, 24 + n:32 + n],
                           in1=fixt[:, g * 16 + 8:g * 16 + 16], op=ALU.mult)
                self.V("scalar_tensor_tensor", [win.b, mT.b], [dT.b], out=dT[:, cc, :], in0=win[:, 32:32 + n],
                       scalar=1.0 / w, in1=mT[:, 32:32 + n], op0=ALU.mult, op1=ALU.subtract)
                P.barrier()
                P.release(m1)
            if self.dbg and kind == "x" and b == 0:
                dd_ = self.dbg_d
                tmpd = self.T([128, 8, n], F32, "tmpd")
                self.V("tensor_copy", [dT.b], [tmpd.b], out=tmpd[:], in_=dT[:])
                self.DMA("sp", [tmpd.b], [d["dbg_d"]], dd_.rearrange("c p t -> p c t"), tmpd[:])
            m2T = self.T([128, 8, n], BF16, "pm2T")
            for ec in range(8):
                g = ec // 2
                e0 = (ec % 2) * 128
                bk = 4 + ec % 2
                for c in range(2):
                    self.M("matmul", [wg.b, dT.b], [pb[bk]], ps[:, bk, 0:n], lhsT=wg[:, g, c, e0:e0 + 128],
                           rhs=dT[:, g * 2 + c, :], start=(c == 0), stop=(c == 1))
                self.A("activation", [pb[bk], psc.b], [m2T.b], out=m2T[:, ec, :], in_=ps[:, bk, 0:n], func=AF.Identity,
                       scale=psc[:, ec:ec + 1])
            if self.dbg and kind == "x" and b == 0:
                tmpe = self.T([128, 8, n], F32, "tmpe")
                self.V("tensor_copy", [m2T.b], [tmpe.b], out=tmpe[:], in_=m2T[:])
                self.DMA("sp", [tmpe.b], [d["dbg_m2"]], self.dbg_m2.rearrange("c p t -> p c t"), tmpe[:])
            for tt in range(ntile):
                for nb in range(2):
                    bk = 6 + nb
                    for c in range(8):
                        self.M("matmul", [m2T.b, wo.b], [pb[bk]], ps[:, bk, :], lhsT=m2T[:, c, tt * 128:(tt + 1) * 128],
                               rhs=wo[:, c, nb * 512:(nb + 1) * 512], start=(c == 0), stop=(c == 7))
                out_t = self.T([128, D], F32, "pxs1")
                md = self.mod[kind]
                self.ln_residual(ps[:, 6:8, :].rearrange("p a b -> p (a b)"), [pb[6], pb[7]], xs_tiles[tt],
                                 md[:, 2 * D:3 * D], md.b, 0, out_t)
                row0 = (t0 + tt) * 128 if kind == "x" else NQ + tt * 128
                self.DMA("sp", [out_t.b], [d["xs1d"]], self.xs1d[row0:row0 + 128, :], out_t[:])
            P.barrier()
            P.release(base)

    def peer_phase(self):
        nc, P, ps, pb, d = self.nc, self.P, self.ps, self.pb, self.dbufs
        ident = self.ident
        keysT = self.T([128, 16, 128], F32, "keysT")
        m0 = P.mark()
        for hp4 in range(4):
            kt = self.T([128, 4, 128], F32, f"ktmp{hp4}")
            self.DMA("sp", [d["peer_keys"]], [kt.b], kt[:],
                     self.keys_d[hp4 * 4:(hp4 + 1) * 4, :, :].rearrange("h k d -> k h d"))
            for j in range(4):
                self.M("transpose", [kt.b, ident.b], [pb[hp4 % 2]], out=ps[:, hp4 % 2, j * 128:(j + 1) * 128],
                       in_=kt[:, j, :], identity=ident[:])
            self.A("copy", [pb[hp4 % 2]], [keysT.b], out=keysT[:, hp4 * 4:(hp4 + 1) * 4, :],
                   in_=ps[:, hp4 % 2, :].rearrange("p (h k) -> p h k", k=128))
        P.barrier()
        P.release(m0)
        wqr = [self.T([128, 8, 128], F32, f"wqb{k}") for k in range(4)]
        wq_v = self.wq_d.rearrange("(c p) n -> p c n", p=128)
        hx2 = self.T([128, D], F32, "hx2")
        hx2T = self.T([128, 8, 128], F32, "hx2T")
        qTr = [self.T([128, 128], F32, f"qT{k}") for k in range(2)]
        S = self.T([128, 16, 128], F32, "S")
        Sw = self.T([128, 128], F32, "Sw")
        top = self.T([128, 8, 2, 16], F32, "top")
        topi = self.T([128, 8, 2, 16], U32, "topi")
        topf = self.T([128, 8, 2, 16], F32, "topf")
        cand = self.T([128, 8, 256], F32, "cand")
        candw = self.T([128, 256], F32, "candw")
        best = self.T([128, 8, 16], F32, "best")
        pos = self.T([128, 8, 16], U32, "pos")
        pa = self.T([128, 2, 128], U32, "pa")
        paf = self.T([128, 2, 128], F32, "paf")
        oh = self.T([128, 8, 16, 16], F32, "oh")
        i12 = self.T([128, 2, 128], F32, "i12")
        idxf = self.T([128, 128], F32, "idxf")
        idxr = [self.T([128, 128], U32, f"idx{k}") for k in range(2)]
        gate = self.T([128, 8, 16], F32, "gate")
        gsm = self.T([128, 8], F32, "gsm")
        actr = [self.T([128, 128], F32, f"act{k}") for k in range(2)]
        wgt = [self.T([128, 128], F32, f"wgt{k}") for k in range(2)]
        NU = 4
        ubuf = [self.T([128, D], F32, f"ub{k}") for k in range(NU)]
        vbuf = [self.T([128, D], F32, f"vb{k}") for k in range(NU)]
        junk = self.T([128, D], BF16, "pjunk")
        dgr = [self.T([128, 128], F32, f"dg{k}") for k in range(4)]
        xs1r = [self.T([128, D], F32, f"xs1_{k}") for k in range(2)]
        outr = [self.T([128, D], F32, f"xo_{k}") for k in range(2)]
        tiles = self.tile_list()
        ui = 0
        vi = 0
        wqi = 0
        for ti, (kind, t) in enumerate(tiles):
            md = self.mod[kind]
            row0 = t * 128 if kind == "x" else NQ + t * 128
            xs1 = xs1r[ti % 2]
            self.DMA("sp", [d["xs1d"]], [xs1.b], xs1[:], self.xs1d[row0:row0 + 128, :])
            self.V("tensor_tensor", [xs1.b, md.b], [hx2.b], out=hx2[:], in0=xs1[:], in1=md[:, 4 * D:5 * D], op=ALU.mult)
            self.V("tensor_tensor", [hx2.b, md.b], [hx2.b], out=hx2[:], in0=hx2[:], in1=md[:, 3 * D:4 * D], op=ALU.add)
            for c in range(8):
                bk = 2 + c // 4
                o = (c % 4) * 128
                self.M("transpose", [hx2.b, ident.b], [pb[bk]], out=ps[:, bk, o:o + 128], in_=hx2[:, c * 128:(c + 1) * 128],
                       identity=ident[:])
            for bk in (2, 3):
                self.A("copy", [pb[bk]], [hx2T.b], out=hx2T[:, (bk - 2) * 4:(bk - 1) * 4, :],
                       in_=ps[:, bk, :].rearrange("p (c t) -> p c t", t=128))
            for hp in range(16):
                wt = wqr[wqi % 4]
                wqi += 1
                self.DMA("act", [d["peer_wq"]], [wt.b], wt[:], wq_v[:, :, hp * 128:(hp + 1) * 128])
                bq = 4 + hp % 2
                for c in range(8):
                    self.M("matmul", [wt.b, hx2T.b], [pb[bq]], ps[:, bq, 0:128], lhsT=wt[:, c, :], rhs=hx2T[:, c, :],
                           start=(c == 0), stop=(c == 7))
                qT = qTr[hp % 2]
                self.A("copy", [pb[bq]], [qT.b], out=qT[:], in_=ps[:, bq, 0:128])
                bs_ = 6 + hp % 2
                self.M("matmul", [qT.b, keysT.b], [pb[bs_]], ps[:, bs_, 0:128], lhsT=qT[:], rhs=keysT[:, hp, :],
                       start=True, stop=True)
                self.A("copy", [pb[bs_]], [S.b], out=S[:, hp, :], in_=ps[:, bs_, 0:128])
            for hp in range(16):
                h, p_ = hp // 2, hp % 2
                self.V("max", [S.b], [top.b], out=top[:, h, p_, 0:8], in_=S[:, hp, :])
                self.V("match_replace", [top.b, S.b], [Sw.b], out=Sw[:], in_to_replace=top[:, h, p_, 0:8],
                       in_values=S[:, hp, :], imm_value=-1e30)
                self.V("max", [Sw.b], [top.b], out=top[:, h, p_, 8:16], in_=Sw[:])
                self.V("max_index", [top.b, S.b], [topi.b], out=topi[:, h, p_, 0:8], in_max=top[:, h, p_, 0:8],
                       in_values=S[:, hp, :])
                self.V("max_index", [top.b, S.b], [topi.b], out=topi[:, h, p_, 8:16], in_max=top[:, h, p_, 8:16],
                       in_values=S[:, hp, :])
            self.V("tensor_copy", [topi.b], [topf.b], out=topf[:].rearrange("p h t k -> p (h t k)"),
                   in_=topi[:].rearrange("p h t k -> p (h t k)"))
            for h in range(8):
                self.V("tensor_tensor", [top.b], [cand.b], out=cand[:, h, :].rearrange("p (a b) -> p a b", b=16),
                       in0=top[:, h, 0, :].unsqueeze(2).to_broadcast([128, 16, 16]),
                       in1=top[:, h, 1, :].unsqueeze(1).to_broadcast([128, 16, 16]), op=ALU.add)
                self.V("max", [cand.b], [best.b], out=best[:, h, 0:8], in_=cand[:, h, :])
                self.V("match_replace", [best.b, cand.b], [candw.b], out=candw[:], in_to_replace=best[:, h, 0:8],
                       in_values=cand[:, h, :], imm_value=-1e30)
                self.V("max", [candw.b], [best.b], out=best[:, h, 8:16], in_=candw[:])
                self.V("max_index", [best.b, cand.b], [pos.b], out=pos[:, h, 0:8], in_max=best[:, h, 0:8],
                       in_values=cand[:, h, :])
                self.V("max_index", [best.b, cand.b], [pos.b], out=pos[:, h, 8:16], in_max=best[:, h, 8:16],
                       in_values=cand[:, h, :])
            posf = pos[:].rearrange("p h k -> p (h k)")
            self.V("tensor_single_scalar", [pos.b], [pa.b], out=pa[:, 0, :], in_=posf, scalar=4,
                   op=ALU.logical_shift_right)
            self.V("tensor_single_scalar", [pos.b], [pa.b], out=pa[:, 1, :], in_=posf, scalar=15, op=ALU.bitwise_and)
            self.V("tensor_copy", [pa.b], [paf.b], out=paf[:].rearrange("p t k -> p (t k)"),
                   in_=pa[:].rearrange("p t k -> p (t k)"))
            for t_ in range(2):
                for h in range(8):
                    self.V("tensor_tensor", [paf.b, self.iota16.b], [oh.b], out=oh[:, h, :, :],
                           in0=paf[:, t_, h * 16:(h + 1) * 16].unsqueeze(2).to_broadcast([128, 16, 16]),
                           in1=self.iota16[:].unsqueeze(1).to_broadcast([128, 16, 16]), op=ALU.is_equal)
                    self.V("tensor_tensor", [oh.b, topf.b], [oh.b], out=oh[:, h, :, :], in0=oh[:, h, :, :],
                           in1=topf[:, h, t_, :].unsqueeze(1).to_broadcast([128, 16, 16]), op=ALU.mult)
                self.V("tensor_reduce", [oh.b], [i12.b], out=i12[:, t_, :],
                       in_=oh[:].rearrange("p h k a -> p (h k) a"), axis=AX.X, op=ALU.add)
            self.V("scalar_tensor_tensor", [i12.b], [idxf.b], out=idxf[:], in0=i12[:, 0, :], scalar=128.0,
                   in1=i12[:, 1, :], op0=ALU.mult, op1=ALU.add)
            idx = idxr[ti % 2]
            self.V("tensor_copy", [idxf.b], [idx.b], out=idx[:], in_=idxf[:])
            self.V("tensor_tensor", [best.b], [gate.b], out=gate[:], in0=best[:],
                   in1=best[:, :, 0:1].to_broadcast([128, 8, 16]), op=ALU.subtract)
            self.A("activation", [gate.b], [gate.b], out=gate[:].rearrange("p h k -> p (h k)"),
                   in_=gate[:].rearrange("p h k -> p (h k)"), func=AF.Exp)
            self.V("tensor_reduce", [gate.b], [gsm.b], out=gsm[:], in_=gate[:], axis=AX.X, op=ALU.add)
            self.V("reciprocal", [gsm.b], [gsm.b], out=gsm[:], in_=gsm[:])
            self.V("tensor_tensor", [gate.b, gsm.b], [gate.b], out=gate[:], in0=gate[:],
                   in1=gsm[:].unsqueeze(2).to_broadcast([128, 8, 16]), op=ALU.mult)
            act = actr[ti % 2]
            for c in range(128):
                ub = ubuf[ui % NU]
                ui += 1
                self.P.dma("pool", self._gather(ub, self.u_d, idx, c), [idx.b, d["peer_u"]], [ub.b])
                self.V("scalar_tensor_tensor", [ub.b, hx2.b], [junk.b, act.b], out=junk[:], in0=ub[:], scalar=1.0,
                       in1=hx2[:], op0=ALU.mult, op1=ALU.mult, accum_out=act[:, c:c + 1])
            wg_ = wgt[ti % 2]
            self.A("activation", [act.b], [act.b], out=act[:], in_=act[:], func=AF.Gelu_apprx_tanh)
            self.V("tensor_tensor", [act.b, gate.b], [wg_.b], out=wg_[:], in0=act[:],
                   in1=gate[:].rearrange("p h k -> p (h k)"), op=ALU.mult)
            for c in range(128):
                vb = vbuf[vi % NU]
                dg = dgr[vi % 4]
                vi += 1
                self.P.dma("pool", self._gather(vb, self.v_d, idx, c), [idx.b, d["peer_v"]], [vb.b])
                self.A("activation", [ident.b, wg_.b], [dg.b], out=dg[:], in_=ident[:], func=AF.Identity,
                       scale=wg_[:, c:c + 1])
                for nb in range(2):
                    self.M("matmul", [dg.b, vb.b], [pb[nb]], ps[:, nb, :], lhsT=dg[:], rhs=vb[:, nb * 512:(nb + 1) * 512],
                           start=(c == 0), stop=(c == 127))
            out_t = outr[ti % 2]
            self.ln_residual(ps[:, 0:2, :].rearrange("p a b -> p (a b)"), [pb[0], pb[1]], xs1, md[:, 5 * D:6 * D],
                             md.b, 1, out_t)
            if kind == "x":
                self.DMA("sp", [out_t.b], [d["xs_out"]], self.xs_out[t * 128:(t + 1) * 128, :], out_t[:])
            else:
                self.DMA("sp", [out_t.b], [d["cs_out"]], self.cs_out[t * 128:(t + 1) * 128, :], out_t[:])

    def _gather(self, dst, table, idx, c):
        nc = self.nc
        return lambda: nc.gpsimd.indirect_dma_start(
            out=dst[:], out_offset=None, in_=table[:, :],
            in_offset=bass.IndirectOffsetOnAxis(ap=idx[:, c:c + 1], axis=0))


class Tlv:
    def __init__(self, tl, k):
        self.tl = tl
        self.k = k
        self.b = tl.b

    def __getitem__(self, key):
        return self.tl.t[:, self.k:self.k + 1]


_PROGS = {}


def _rope_tables():
    t = np.arange(SEQ)
    row = (t // 64).astype(np.float32)
    col = (t % 64).astype(np.float32)
    inv = (10000.0 ** (-np.arange(16, dtype=np.float32) / 16)).astype(np.float32)
    ang = np.concatenate([row[:, None] * inv, col[:, None] * inv], axis=-1).astype(np.float32)
    cos = np.cos(ang).astype(np.float32)
    sin = np.sin(ang).astype(np.float32)
    p = np.arange(128) % 32
    return np.ascontiguousarray(cos[:, p].T), np.ascontiguousarray(sin[:, p].T)


def _fix_table(T, left, right):
    out = np.ones((4, 16), np.float32)
    for g in range(4):
        w = 2 << g
        for j in range(8):
            if left:
                lo, hi = max(j - w // 2, 0), min(j + w // 2, T)
                out[g, j] = w / (hi - lo)
            if right:
                pos_ = T - 8 + j
                lo, hi = max(pos_ - w // 2, 0), min(pos_ + w // 2, T)
                out[g, 8 + j] = w / (hi - lo)
    return out.reshape(64)


def _layer_inputs(li, core, xs, cs, inp, tabs):
    b, qtr = core // 4, core % 4
    j = li // 2
    q0 = qtr * NQ
    f = np.ascontiguousarray
    m = {
        "xs_in": f(xs[b, q0:q0 + NQ]),
        "c_in": f(inp["c"][b].reshape(8, 128)),
        "cc_in": f(inp["c_ctx"].reshape(8, 128)),
        "ada_w": inp["ada_w"][li],
        "ada_b": inp["ada_b"][li],
        "ln_g": inp["ln_g"][li],
        "ln_b": inp["ln_b"][li],
        "peer_wq": inp["peer_wq"][li],
        "peer_keys": f(inp["peer_keys"][li].reshape(16, 128, 128)),
        "peer_u": inp["peer_u"][li],
        "peer_v": inp["peer_v"][li],
    }
    if li <= 2:
        m["cs_in"] = f(cs[b])
    if li % 2 == 0:
        cosT, sinT = tabs
        m.update({
            "xs_full": f(xs[b]),
            "ab_w_in": inp["ab_w_in"][j], "ab_w_out": inp["ab_w_out"][j],
            "diff_lam": f(inp["diff_lam"][j].reshape(256)), "diff_norm_g": inp["diff_norm_g"][j],
            "sgu_ln_g": inp["sgu_ln_g"][j], "sgu_ln_b": inp["sgu_ln_b"][j],
            "sgu_w": inp["sgu_w"][j], "sgu_b": inp["sgu_b"][j],
            "cosk": cosT, "sink": sinT,
            "cosq": f(cosT[:, q0:q0 + NQ]), "sinq": f(sinT[:, q0:q0 + NQ]),
        })
    else:
        halo = np.zeros((64, D), np.float32)
        hv = np.zeros(64, np.float32)
        if qtr > 0:
            halo[0:32] = xs[b, q0 - 32:q0]
            hv[0:32] = 1.0
        if qtr < 3:
            halo[32:64] = xs[b, q0 + NQ:q0 + NQ + 32]
            hv[32:64] = 1.0
        m.update({
            "halo": halo, "halo_valid": hv,
            "fix_x": _fix_table(SEQ, qtr == 0, qtr == 3), "fix_c": _fix_table(CTX, True, True),
            "pool_w_in": inp["pool_w_in"][j], "pool_w_grp": inp["pool_w_grp"][j],
            "pool_scale": f(inp["pool_scale"][j].reshape(8, 128)), "pool_w_out": inp["pool_w_out"][j],
        })
    return m


def kernel(**inputs):
    inp = {k: np.asarray(v, dtype=np.float32) for k, v in inputs.items()}
    xs = inp["x"]
    cs = inp["ctx"]
    tabs = _rope_tables()
    for li in range(DEPTH):
        if li not in _PROGS:
            lp = LayerProg(li)
            _PROGS[li] = (lp, lp.build())
        lp, nc = _PROGS[li]
        in_maps = []
        for core in range(8):
            m = _layer_inputs(li, core, xs, cs, inp, tabs)
            in_maps.append({k: m[k] for k in lp.in_names})
        res = run_bass_kernel_spmd(nc, in_maps, core_ids=list(range(8)))
        new_xs = np.empty_like(xs)
        for core in range(8):
            b, qtr = core // 4, core % 4
            new_xs[b, qtr * NQ:(qtr + 1) * NQ] = res.results[core]["xs_out"]
        if li < 2:
            cs = np.stack([res.results[0]["cs_out"], res.results[4]["cs_out"]], axis=0)
        xs = new_xs
    return xs.astype(np.float32)
```
